# Optimizing a Trainium2 kernel written in Bass

```python
import jax, jax.numpy as jnp
from jax import lax
import numpy as np

D_MODEL = 2048
BATCH = 4
SEQ = 4096
DEPTH = 4

D_FF = 5632
POOL_WINDOWS = (2, 4, 8, 16)
N_POOL_GROUPS = len(POOL_WINDOWS)
POOL_WIDTH = 1024
POOL_GROUP_DIM = POOL_WIDTH // N_POOL_GROUPS
SGU_HEADS = 8
SGU_HEAD_DIM = 128
SGU_WIDTH = SGU_HEADS * SGU_HEAD_DIM
CHUNK = 128
IN_PROJ_WIDTH = POOL_WIDTH + 2 * SGU_WIDTH + 2 * D_MODEL
MACARON_WEIGHT = 0.5
EPS = 1e-6

kernel_name = "hybrid_pool_sgu_macaron_block"


def rmsnorm(x, g):
    xf = x.astype(jnp.float32)
    var = jnp.mean(xf * xf, axis=-1, keepdims=True)
    return (xf * lax.rsqrt(var + EPS)).astype(x.dtype) * g


def swiglu(h, w_up, w_down):
    gate, up = jnp.split(h @ w_up, 2, axis=-1)
    return (jax.nn.silu(gate) * up) @ w_down


def pool_mixer(p, w_group, scale):
    B, S, _ = p.shape
    maxw = POOL_WINDOWS[-1]
    pf = p.astype(jnp.float32)
    cs = jnp.cumsum(pf, axis=1)
    cs_pad = jnp.pad(cs, ((0, 0), (maxw, 0), (0, 0)))
    pos = jnp.arange(1, S + 1, dtype=jnp.int32)
    outs = []
    for g, w in enumerate(POOL_WINDOWS):
        sl = slice(g * POOL_GROUP_DIM, (g + 1) * POOL_GROUP_DIM)
        prev = cs_pad[:, maxw - w: maxw - w + S, sl]
        cnt = jnp.minimum(pos, w).astype(jnp.float32)[None, :, None]
        outs.append((cs[:, :, sl] - prev) / cnt - pf[:, :, sl])
    d = jnp.stack(outs, axis=2).astype(p.dtype)
    y = jnp.einsum('bsgc,gcd->bsgd', d, w_group)
    return y.reshape(B, S, POOL_WIDTH) * scale


def spatial_gating(u, v, v_gain, w_s, b_s):
    B, S, _ = u.shape
    n_chunks = S // CHUNK
    v = rmsnorm(v, v_gain)
    vc = v.reshape(B, n_chunks, CHUNK, SGU_HEADS, SGU_HEAD_DIM)
    w = w_s * jnp.tril(jnp.ones((CHUNK, CHUNK), dtype=w_s.dtype))
    s = jnp.einsum('hts,bnshc->bnthc', w, vc) + b_s.T[None, None, :, :, None]
    return u * s.reshape(B, S, SGU_WIDTH)


def setup_inputs(seed: int = 0) -> dict:
    key = jax.random.key(seed)
    ks = jax.random.split(key, 32)
    L, D = DEPTH, D_MODEL

    def dense(k, shape, fan_in):
        return jax.random.normal(k, shape, jnp.float32) * (fan_in ** -0.5)

    def gain(k, shape):
        return 1.0 + 0.05 * jax.random.normal(k, shape, jnp.float32)

    return {
        "x": jax.random.normal(ks[0], (BATCH, SEQ, D), jnp.float32),
        "g_ffn1_pre": gain(ks[1], (L, D)),
        "w_ffn1_up": dense(ks[2], (L, D, 2 * D_FF), D),
        "w_ffn1_down": dense(ks[3], (L, D_FF, D), D_FF),
        "g_ffn1_post": gain(ks[4], (L, D)),
        "g_mix_pre": gain(ks[5], (L, D)),
        "w_in": dense(ks[6], (L, D, IN_PROJ_WIDTH), D),
        "pool_group_w": dense(ks[7], (L, N_POOL_GROUPS, POOL_GROUP_DIM, POOL_GROUP_DIM), POOL_GROUP_DIM),
        "pool_scale": gain(ks[8], (L, POOL_WIDTH)),
        "w_pool_out": dense(ks[9], (L, POOL_WIDTH, D), POOL_WIDTH),
        "sgu_v_gain": gain(ks[10], (L, SGU_WIDTH)),
        "sgu_w_s": dense(ks[11], (L, SGU_HEADS, CHUNK, CHUNK), CHUNK),
        "sgu_b_s": gain(ks[12], (L, SGU_HEADS, CHUNK)),
        "w_sgu_out": dense(ks[13], (L, SGU_WIDTH, D), SGU_WIDTH),
        "w_out": dense(ks[14], (L, D, D), D),
        "g_mix_post": gain(ks[15], (L, D)),
        "g_ffn2_pre": gain(ks[16], (L, D)),
        "w_ffn2_up": dense(ks[17], (L, D, 2 * D_FF), D),
        "w_ffn2_down": dense(ks[18], (L, D_FF, D), D_FF),
        "g_ffn2_post": gain(ks[19], (L, D)),
    }


def reference(x, g_ffn1_pre, w_ffn1_up, w_ffn1_down, g_ffn1_post, g_mix_pre, w_in,
              pool_group_w, pool_scale, w_pool_out, sgu_v_gain, sgu_w_s, sgu_b_s,
              w_sgu_out, w_out, g_mix_post, g_ffn2_pre, w_ffn2_up, w_ffn2_down, g_ffn2_post):
    splits = (POOL_WIDTH, POOL_WIDTH + SGU_WIDTH, POOL_WIDTH + 2 * SGU_WIDTH,
              POOL_WIDTH + 2 * SGU_WIDTH + D_MODEL)
    for i in range(DEPTH):
        f = swiglu(rmsnorm(x, g_ffn1_pre[i]), w_ffn1_up[i], w_ffn1_down[i])
        x = x + MACARON_WEIGHT * rmsnorm(f, g_ffn1_post[i])

        h = rmsnorm(x, g_mix_pre[i])
        p, u, v, ga, gb = jnp.split(h @ w_in[i], splits, axis=-1)
        y_a = pool_mixer(p, pool_group_w[i], pool_scale[i]) @ w_pool_out[i]
        y_b = spatial_gating(jax.nn.gelu(u), jax.nn.gelu(v), sgu_v_gain[i],
                             sgu_w_s[i], sgu_b_s[i]) @ w_sgu_out[i]
        m = jax.nn.sigmoid(ga) * y_a + jax.nn.sigmoid(gb) * y_b
        x = x + rmsnorm(m @ w_out[i], g_mix_post[i])

        f = swiglu(rmsnorm(x, g_ffn2_pre[i]), w_ffn2_up[i], w_ffn2_down[i])
        x = x + MACARON_WEIGHT * rmsnorm(f, g_ffn2_post[i])
    return x
```

```python
import numpy as np
from contextlib import ExitStack

import concourse.bass as bass
import concourse.mybir as mybir
from concourse.bass_utils import run_bass_kernel_spmd

F32 = mybir.dt.float32
BF16 = mybir.dt.bfloat16
AF = mybir.ActivationFunctionType
ALU = mybir.AluOpType

D = 2048
DFF = 5632
KD = D // 128
NJ = DFF // 128
NJH = NJ // 2
PW = 1024
DEPTH = 4
EPS = 1e-6
NT = 18
TP = 6
NPASS = NT // TP
G = TP * 128
H2 = G // 2
NSLOT = 6
SLOT_E = 2048
N_CORES = 8
FUSED = True
DEBUG = {}
HALO_SKIP = ((0, 0, 112, 112), (112, 0, 128, 128), (128, 128, 240, 240), (240, 128, 256, 256))


def _esz(dt):
    return 4 if dt is F32 else 2


class Region:
    def __init__(self, name, handle, nbytes):
        self.name, self.h, self.nbytes = name, handle, nbytes


class T:
    def __init__(self, region, dtype, off, dims, rows=(0, 128)):
        self.region, self.dtype, self.off, self.dims, self.rows = region, dtype, off, tuple(dims), rows
        es = _esz(dtype)
        n = int(np.prod(dims))
        assert off % 4 == 0 and off + n * es <= region.nbytes, (region.name, off, n * es, region.nbytes)
        base = region.h[rows[0]:rows[1], off // 2: off // 2 + n * es // 2]
        if dtype is F32:
            base = base.bitcast(F32)
        if len(dims) > 1:
            names = [f"a{i}" for i in range(len(dims))]
            pat = "p (" + " ".join(names) + ") -> p " + " ".join(names)
            base = base.rearrange(pat, **{nm: d for nm, d in zip(names[:-1], dims[:-1])})
        self.ap = base
        st = [1] * len(dims)
        for i in range(len(dims) - 2, -1, -1):
            st[i] = st[i + 1] * dims[i + 1]
        self.strides = st

    def __call__(self, *idx):
        idx = list(idx) + [slice(None)] * (len(self.dims) - len(idx))
        lo = hi = 0
        for i, d, s in zip(idx, self.dims, self.strides):
            if isinstance(i, int):
                a, b = i, i + 1
            else:
                a, b = i.indices(d)[:2]
            lo += a * s
            hi += (b - 1) * s
        es = _esz(self.dtype)
        res = (self.region.name, self.off + lo * es, self.off + (hi + 1) * es)
        return (self.ap[(slice(None),) + tuple(idx)], res)


class Sched:
    ENGS = ("pe", "act", "dve", "pool", "sp")

    def __init__(self, nc, es):
        self.nc, self.es = nc, es
        self.q = {e: [] for e in self.ENGS}
        self.sems = {}
        self.cnt = {}
        self.known = {e: {} for e in self.ENGS}
        self.segs = {}
        for e in self.ENGS:
            self.sem("E_" + e)

    def sem(self, name):
        if name not in self.sems:
            self.sems[name] = self.es.enter_context(self.nc.semaphore(name))
            self.cnt[name] = 0
        return self.sems[name]

    def add_region(self, name, nbytes):
        self.segs[name] = [[0, nbytes, None, {}]]

    def _range(self, res):
        name, s, e = res
        segs = self.segs[name]
        for x in (s, e):
            for i, sg in enumerate(segs):
                if sg[0] < x < sg[1]:
                    segs.insert(i + 1, [x, sg[1], sg[2], dict(sg[3])])
                    sg[1] = x
                    break
        return [sg for sg in segs if sg[0] >= s and sg[1] <= e]

    def _deps(self, eng, reads, writes):
        need = {}

        def add(ev):
            if ev is not None and need.get(ev[0], 0) < ev[1]:
                need[ev[0]] = ev[1]

        for r in reads:
            for sg in self._range(r):
                add(sg[2])
        for w in writes:
            for sg in self._range(w):
                add(sg[2])
                for s, v in sg[3].items():
                    add((s, v))
        waits = []
        for s, v in need.items():
            if eng == "pe" and s == "E_pe":
                continue
            if self.known[eng].get(s, 0) >= v:
                continue
            self.known[eng][s] = v
            waits.append((s, v))
        return waits

    def _commit(self, ev, reads, writes):
        for r in reads:
            for sg in self._range(r):
                if sg[3].get(ev[0], 0) < ev[1]:
                    sg[3][ev[0]] = ev[1]
        for w in writes:
            name, s, e = w
            self._range(w)
            segs = self.segs[name]
            keep = [sg for sg in segs if not (sg[0] >= s and sg[1] <= e)]
            keep.append([s, e, ev, {}])
            keep.sort(key=lambda sg: sg[0])
            self.segs[name] = keep

    def op(self, eng, fn, reads=(), writes=()):
        reads = [r for r in reads if r is not None]
        waits = self._deps(eng, reads, writes)
        s = "E_" + eng
        self.cnt[s] += 1
        ev = (s, self.cnt[s])
        sem = self.sems[s]
        self.q[eng].append((waits, lambda e, fn=fn, sem=sem: fn(e).then_inc(sem, 1)))
        self._commit(ev, reads, writes)

    def dma(self, eng, pairs, semname, reads=(), writes=()):
        sem = self.sem(semname)
        waits = self._deps(eng, reads, writes)
        self.cnt[semname] += 16 * len(pairs)
        ev = (semname, self.cnt[semname])

        def fn(e, pairs=pairs, sem=sem):
            for dst, src in pairs:
                e.dma_start(out=dst, in_=src).then_inc(sem, 16)

        self.q[eng].append((waits, fn))
        self._commit(ev, reads, writes)
        return ev

    def wait_event(self, eng, ev):
        self.q[eng].append(([ev], lambda e: None))

    def replay(self, block):
        sems = self.sems

        def run(e, items):
            for waits, fn in items:
                for s, v in waits:
                    e.wait_ge(sems[s], v)
                fn(e)

        @block.tensor
        def _(e):
            run(e, self.q["pe"])

        @block.scalar
        def _(e):
            run(e, self.q["act"])

        @block.vector
        def _(e):
            run(e, self.q["dve"])

        @block.gpsimd
        def _(e):
            run(e, self.q["pool"])

        @block.sync
        def _(e):
            run(e, self.q["sp"])


def build_program(nl, npass=NPASS, final_tiles=None):
    nt = npass * TP
    ntok = nt * 128
    nc = bass.Bass("TRN2", target_bir_lowering=False)

    def din(name, shape):
        return nc.dram_tensor(name, list(shape), F32, kind="ExternalInput").ap()

    xT_d = din("xT", (D, ntok))
    w_up = [din("w_up1", (nl, D, 2 * DFF)), din("w_up2", (nl, D, 2 * DFF))]
    w_dn = [din("w_dn1", (nl, DFF, D)), din("w_dn2", (nl, DFF, D))]
    w_in = din("w_in", (nl, D, 3 * PW + 2 * D))
    pgw = din("pgw", (nl, 4, 256, 256))
    w_po = din("w_po", (nl, PW, D))
    w_so = din("w_so", (nl, PW, D))
    w_o = din("w_o", (nl, D, D))
    gcols_d = din("gcols", (nl, 128, 6 * KD))
    pscale_d = din("pscale", (nl, 128, 8))
    vgain_d = din("vgain", (nl, 128, PW))
    wsT_d = din("wsT", (nl, 128, PW))
    bs_d = din("bs", (nl, 1, PW))
    A_d = din("Amat", (128, 16 * 128))
    mask_d = din("mask", (128, PW))
    oT_d = nc.dram_tensor("oT", [D, ntok], F32, kind="ExternalOutput").ap()
    dbg_d = nc.dram_tensor("dbg", [128, KD * G], F32, kind="ExternalOutput").ap() if DEBUG else None

    with ExitStack() as es:
        S = Sched(nc, es)

        def region(name, nbytes):
            h = es.enter_context(nc.sbuf_tensor(name, [128, nbytes // 2], BF16))
            S.add_region(name, nbytes)
            return Region(name, h, nbytes)

        R_x = region("xT_sb", KD * G * 4)
        R_h = region("hT_sb", KD * G * 2)
        R_f = region("fT_sb", KD * G * 4)
        R_big = region("big_sb", NJH * G * 2)
        R_w = region("wring", NSLOT * SLOT_E * 2)
        R_m = region("misc", 3072 + 3072 + 4 * 1536 + 3072 + 2048 + 2 * 384 + 32 + 256 + 64 + 2048 * nl + 224)
        R_b = region("brow", 2048)
        R_b2 = region("brow2", 2048)
        ps_h = es.enter_context(nc.psum_tensor("psum", [128, 4096], F32))
        S.add_region("psum", 16384)

        class PS:
            name, h, nbytes = "psum", ps_h, 16384

        def ps(bank, n, off=0):
            a, b = bank * 512 + off, bank * 512 + off + n
            return (ps_h[:, a:b], ("psum", a * 4, b * 4))

        xT = T(R_x, F32, 0, (KD, G))
        hT = T(R_h, BF16, 0, (KD, G))
        fT = T(R_f, F32, 0, (KD, G))

        vn = T(R_f, BF16, 0, (TP, PW))
        gT = T(R_f, BF16, 12288, (8, G))
        bhf_stage = T(R_f, F32, 12288, (PW,), rows=(0, 1))
        jk_junk = T(R_f, BF16, 24576, (PW,))
        vgain = T(R_f, F32, 26624, (PW,))
        Amat = T(R_f, BF16, 30720, (16, 128))
        ya = T(R_f, BF16, 36864, (8, G))
        wst_stage = T(R_f, F32, 36864, (PW,))
        mask_stage = T(R_f, F32, 40960, (PW,))
        b_stage = T(R_f, F32, 45056, (PW,), rows=(0, 1))
        actT = T(R_big, BF16, 0, (NJH, G))
        p_tm = T(R_big, BF16, 0, (TP, PW))
        dT = T(R_big, BF16, 12288, (8, G))
        gv = T(R_big, F32, 0, (TP, PW))
        mT = T(R_big, BF16, 0, (KD, G))
        mo = 0

        def misc(dtype, dims, rows=(0, 128)):
            nonlocal mo
            t = T(R_m, dtype, mo, dims, rows)
            mo += int(np.prod(dims)) * _esz(dtype)
            mo = (mo + 31) // 32 * 32
            return t

        rstd_bc = misc(F32, (G,))
        sq = misc(BF16, (2, G))
        scr = misc(F32, (4, H2))
        tpost = misc(F32, (G,))
        scr2 = T(R_m, F32, scr.off, (2, G))
        WmT = misc(BF16, (8, 128))
        gcols = misc(F32, (6, KD))
        gcolsB = misc(F32, (6, KD))
        pscale = misc(F32, (8,))
        ones = misc(BF16, (128,))
        small = misc(F32, (16,))
        p_prev = misc(BF16, (nl, PW))
        b_hi = T(R_b, BF16, 0, (PW,), rows=(0, 1))
        b_lo = T(R_b2, BF16, 0, (PW,), rows=(0, 1))
        wslot = T(R_w, BF16, 0, (NSLOT, SLOT_E))

        state = {"slot": 0, "bank": 0, "scr": 0, "sq": 0}

        def fetch(parts):
            slot = state["slot"] % NSLOT
            state["slot"] += 1
            off = 0
            views, pairs, writes = [], [], []
            for src, dims in parts:
                n = int(np.prod(dims))
                v = T(R_w, BF16, (slot * SLOT_E + off) * 2, dims)
                off += n
                assert off <= SLOT_E
                views.append(v)
                ap, res = v()
                pairs.append((ap, src))
                writes.append(res)
            S.dma("pool", pairs, f"W{slot}", writes=writes)
            return views

        def bank():
            b = state["bank"] % 6
            state["bank"] += 1
            return b

        def scr_next():
            i = state["scr"] % 4
            state["scr"] += 1
            return scr(i, slice(0, hw()))

        def mm(out, pairs, start=True, stop=True):
            reads = [x[1] for pr in pairs for x in pr]
            n = len(pairs)

            def fn(e, out=out, pairs=pairs, start=start, stop=stop):
                ins = None
                for i, (l, r) in enumerate(pairs):
                    ins = e.matmul(out[0], lhsT=l[0], rhs=r[0], start=(start and i == 0), stop=(stop and i == n - 1))
                return ins

            S.op("pe", fn, reads=reads, writes=[out[1]])

        def act(out, in_, func, scale=1.0, bias=0.0, accum=None, extra_reads=()):
            reads = [in_[1]] + [x[1] for x in extra_reads]
            writes = [out[1]] + ([accum[1]] if accum is not None else [])
            sc = scale[0] if isinstance(scale, tuple) else scale
            if isinstance(scale, tuple):
                reads.append(scale[1])
            kw = {}
            if accum is not None:
                kw["accum_out"] = accum[0]
            S.op("act", lambda e: e.activation(out=out[0], in_=in_[0], func=func, bias=bias, scale=sc, **kw), reads=reads, writes=writes)

        def tt(out, a, b, op, eng="dve"):
            S.op(eng, lambda e: e.tensor_tensor(out=out[0], in0=a[0], in1=b[0], op=op), reads=[a[1], b[1]], writes=[out[1]])

        def stt(out, in0, scalar, in1, op0, op1, eng="dve"):
            reads = [in0[1], in1[1]]
            sc = scalar
            if isinstance(scalar, tuple):
                reads.append(scalar[1])
                sc = scalar[0]
            S.op(eng, lambda e: e.scalar_tensor_tensor(out=out[0], in0=in0[0], scalar=sc, in1=in1[0], op0=op0, op1=op1), reads=reads, writes=[out[1]])

        def copy(out, in_, eng="dve"):
            S.op(eng, lambda e: e.tensor_copy(out=out[0], in_=in_[0]), reads=[in_[1]], writes=[out[1]])

        def recip(out, in_):
            S.op("dve", lambda e: e.reciprocal(out=out[0], in_=in_[0]), reads=[in_[1]], writes=[out[1]])

        ones_ap = ones()
        cur = {"c0": 0}

        def hw():
            return (G - cur["c0"]) // 2

        def halves():
            c0, h = cur["c0"], (G - cur["c0"]) // 2
            return [slice(c0, c0 + h), slice(c0 + h, G)]

        def FULL():
            return slice(cur["c0"], G)

        class at_c0:
            def __init__(self, c0):
                self.c0 = c0

            def __enter__(self):
                self.old = cur["c0"]
                cur["c0"] = self.c0

            def __exit__(self, *a):
                cur["c0"] = self.old

        S.op("dve", lambda e: e.memset(ones_ap[0], 1.0), writes=[ones_ap[1]])
        for li in range(nl):
            S.op("dve", lambda e, li=li: e.memset(p_prev(li)[0], 0.0), writes=[p_prev(li)[1]])

        def norm_stats_bank(half):
            return 6 + half

        def square_of(tv, idx):
            i = state["sq"] % 2
            state["sq"] += 1
            act(sq(i, FULL()), tv(idx, FULL()), AF.Square)
            return sq(i)

        def stats_mm(s, first, last):
            for half, hs in enumerate(halves()):
                mm(ps(6 + half, hw()), [(ones_ap, (s[0][:, hs], s[1]))], start=first, stop=last)

        def stats_of(tv, idx, first, last):
            stats_mm(square_of(tv, idx), first, last)

        def rstd_from_stats(weight=1.0):
            w2 = 1.0 / (weight * weight)
            for half, hs in enumerate(halves()):
                act(rstd_bc(hs), ps(6 + half, hw()), AF.Sqrt, scale=w2 / D, bias=w2 * EPS)
            recip(rstd_bc(FULL()), rstd_bc(FULL()))

        def pre_chunk(k, style, gc):
            if style == "ffn":
                act(hT(k, FULL()), xT(k, FULL()), AF.Copy, scale=gc(slice(k, k + 1)))
            stats_of(xT, k, k == 0, k == KD - 1)

        def pre_finish(style, gc):
            rstd_from_stats(1.0)
            if style == "mix":
                for k in range(KD):
                    stt(hT(k, FULL()), xT(k, FULL()), gc(slice(k, k + 1)), rstd_bc(FULL()), ALU.mult, ALU.mult)

        def boundary(gpost, weight, nxt, c0):
            with at_c0(c0):
                rstd_from_stats(weight)
            tbs = [lambda sl: tpost(sl), lambda sl: scr2(0, sl), lambda sl: scr2(1, sl)]
            for k in range(KD + 1):
                with at_c0(c0):
                    if k < KD:
                        stt(tbs[k % 3](FULL()), fT(k, FULL()), gpost(slice(k, k + 1)), rstd_bc(FULL()), ALU.mult, ALU.mult)
                    if k > 0:
                        tt(xT(k - 1, FULL()), xT(k - 1, FULL()), tbs[(k - 1) % 3](FULL()), ALU.add)
                if k > 0 and nxt is not None:
                    with at_c0(nxt[2]):
                        pre_chunk(k - 1, nxt[0], nxt[1])
            if nxt is not None:
                with at_c0(nxt[2]):
                    pre_finish(nxt[0], nxt[1])

        def tail(get_units, mm_pairs, evac, fin):
            gpost, weight, nxt, c0 = fin
            split = nxt is not None and nxt[2] == c0
            hv = halves()
            h_ = hw()
            w2 = 1.0 / (weight * weight)
            tbs = [lambda sl: tpost(sl), lambda sl: scr2(0, sl), lambda sl: scr2(1, sl)]
            st = {"pend": None, "n": 0}

            def sq_half(tv, idx, half):
                i = state["sq"] % 2
                state["sq"] += 1
                act(sq(i, hv[half]), tv(idx, hv[half]), AF.Square)
                return sq(i)

            def smm_half(sv, half, first, last):
                mm(ps(6 + half, h_), [(ones_ap, (sv[0][:, hv[half]], sv[1]))], start=first, stop=last)

            def flush():
                if st["pend"] is not None:
                    sv, half, k = st["pend"]
                    smm_half(sv, half, k == 0, k == KD - 1)
                    st["pend"] = None

            def chunk_update(k, half):
                hs = hv[half]
                tb = tbs[st["n"] % 3](hs)
                st["n"] += 1
                stt(tb, fT(k, hs), gpost(slice(k, k + 1)), rstd_bc(hs), ALU.mult, ALU.mult)
                tt(xT(k, hs), xT(k, hs), tb, ALU.add)
                flush()
                if nxt[0] == "ffn":
                    act(hT(k, hs), xT(k, hs), AF.Copy, scale=nxt[1](slice(k, k + 1)))
                st["pend"] = (sq_half(xT, k, half), half, k)

            def rstd_half(half, sc, bi):
                act(rstd_bc(hv[half]), ps(6 + half, h_), AF.Sqrt, scale=sc, bias=bi)
                recip(rstd_bc(hv[half]), rstd_bc(hv[half]))

            HM = 3 if split else 0
            for dc in range(KD - HM):
                units = get_units(dc)
                svs = [sq_half(fT, dc - 1, half) for half in range(2)] if dc > 0 else None
                for half in range(2):
                    b = bank()
                    mm(ps(b, h_), mm_pairs(units, dc, hv[half]))
                    evac(dc, hv[half], ps(b, h_))
                if svs is not None:
                    for half in range(2):
                        smm_half(svs[half], half, dc == 1, False)
            if HM == 0:
                for half in range(2):
                    smm_half(sq_half(fT, KD - 1, half), half, False, True)
                    rstd_half(half, w2 / D, w2 * EPS)
            else:
                tail_units = [get_units(dc) for dc in range(KD - HM, KD)]
                nupd = 0
                for half in range(2):
                    for i, dc in enumerate(range(KD - HM, KD)):
                        sv = sq_half(fT, dc - 1, half)
                        b = bank()
                        mm(ps(b, h_), mm_pairs(tail_units[i], dc, hv[half]))
                        evac(dc, hv[half], ps(b, h_))
                        smm_half(sv, half, dc == 1, False)
                        if half == 1 and i >= 1:
                            for _ in range(2 if i == 1 else 3):
                                chunk_update(nupd, 0)
                                nupd += 1
                    smm_half(sq_half(fT, KD - 1, half), half, False, True)
                    rstd_half(half, w2 / D, w2 * EPS)
            LAG = KD - (nupd if HM else 0)
            if split:
                for k in range(KD - LAG, KD):
                    chunk_update(k, 0)
                flush()
                rstd_half(0, 1.0 / D, EPS)
                for k in range(KD):
                    chunk_update(k, 1)
                flush()
                rstd_half(1, 1.0 / D, EPS)
                if nxt[0] == "mix":
                    for k in range(KD):
                        stt(hT(k, FULL()), xT(k, FULL()), nxt[1](slice(k, k + 1)), rstd_bc(FULL()), ALU.mult, ALU.mult)
            else:
                for k in range(KD + 1):
                    if k < KD:
                        stt(tbs[k % 3](FULL()), fT(k, FULL()), gpost(slice(k, k + 1)), rstd_bc(FULL()), ALU.mult, ALU.mult)
                    if k > 0:
                        tt(xT(k - 1, FULL()), xT(k - 1, FULL()), tbs[(k - 1) % 3](FULL()), ALU.add)
                        if nxt is not None:
                            with at_c0(nxt[2]):
                                pre_chunk(k - 1, nxt[0], nxt[1])
                if nxt is not None:
                    with at_c0(nxt[2]):
                        pre_finish(nxt[0], nxt[1])

        def ffn(li, which, fin):
            wu = w_up[which][li].rearrange("(k p) n -> p k n", p=128)
            wd = w_dn[which][li].rearrange("(j p) n -> p j n", p=128)
            for fh in range(2):
                for jj in range(NJH):
                    j = fh * NJH + jj
                    (wg,) = fetch([(wu[:, :, j * 128:(j + 1) * 128], (KD, 128))])
                    (wv,) = fetch([(wu[:, :, DFF + j * 128:DFF + (j + 1) * 128], (KD, 128))])
                    for half, hs in enumerate(halves()):
                        bg, bu = bank(), bank()
                        mm(ps(bg, hw()), [(wg(k), hT(k, hs)) for k in range(KD)])
                        mm(ps(bu, hw()), [(wv(k), hT(k, hs)) for k in range(KD)])
                        gs, us = scr_next(), scr_next()
                        tt(gs, ps(bg, hw()), rstd_bc(hs), ALU.mult)
                        act(gs, gs, AF.Silu)
                        tt(us, ps(bu, hw()), rstd_bc(hs), ALU.mult)
                        tt(actT(jj, hs), gs, us, ALU.mult)
                if fh == 0:
                    for dc in range(KD):
                        (wa,) = fetch([(wd[:, 0:11, dc * 128:(dc + 1) * 128], (11, 128))])
                        (wb,) = fetch([(wd[:, 11:22, dc * 128:(dc + 1) * 128], (11, 128))])
                        for half, hs in enumerate(halves()):
                            b = bank()
                            mm(ps(b, hw()), [((wa if jj < 11 else wb)(jj % 11), actT(jj, hs)) for jj in range(NJH)])
                            act(fT(dc, hs), ps(b, hw()), AF.Copy)
                else:
                    def get_units(dc):
                        (wa,) = fetch([(wd[:, NJH:NJH + 11, dc * 128:(dc + 1) * 128], (11, 128))])
                        (wb,) = fetch([(wd[:, NJH + 11:NJH + 22, dc * 128:(dc + 1) * 128], (11, 128))])
                        return wa, wb

                    def mm_pairs(units, dc, hs):
                        return [((units[0] if jj < 11 else units[1])(jj % 11), actT(jj, hs)) for jj in range(NJH)]

                    def evac(dc, hs, pv):
                        tt(fT(dc, hs), fT(dc, hs), pv, ALU.add)

                    tail(get_units, mm_pairs, evac, fin)

        def mixer(li, pi, tp0, fin):
            t0 = cur["c0"] // 128
            groups = [g_ for g_ in (list(range(t0, 3)), list(range(max(t0, 3), TP))) if g_]
            wi = w_in[li].rearrange("(k p) n -> p k n", p=128)
            S.dma("sp", [(pscale()[0], pscale_d[li]), (vgain()[0], vgain_d[li]), (wst_stage()[0], wsT_d[li]),
                         (mask_stage()[0], mask_d), (b_stage()[0], bs_d[li])], "PAR2",
                  writes=[pscale()[1], vgain()[1], wst_stage()[1], mask_stage()[1], b_stage()[1]])
            S.dma("pool", [(Amat()[0], A_d.rearrange("p (a b) -> p a b", a=16))], "PAR0", writes=[Amat()[1]])
            wm_flat = T(R_m, BF16, WmT.off, (PW,))
            tt(wm_flat(), wst_stage(), mask_stage(), ALU.mult)
            copy(b_hi(), b_stage())
            copy(bhf_stage(), b_hi())
            tt(b_lo(), b_stage(), bhf_stage(), ALU.subtract)
            for q in range(4):
                (w0,) = fetch([(wi[:, 0:8, q * 256:(q + 1) * 256], (8, 256))])
                (w1,) = fetch([(wi[:, 8:16, q * 256:(q + 1) * 256], (8, 256))])
                for t in range(tp0, TP):
                    b = bank()
                    tsl = slice(t * 128, (t + 1) * 128)
                    mm(ps(b, 256), [(hT(k, tsl), (w0 if k < 8 else w1)(k % 8)) for k in range(KD)])
                    act(p_tm(t, slice(q * 256, (q + 1) * 256)), ps(b, 256), AF.Copy)
            for cc in range(8):
                g = cc // 2
                csl = slice(cc * 128, (cc + 1) * 128)
                for grp in groups:
                    b = bank()
                    for i3, t in enumerate(grp):
                        gt = pi * TP + t
                        prs = []
                        if gt == 2:
                            prs.append((p_tm(t, csl), Amat(8 + g)))
                            prs.append((p_tm(t, csl), Amat(12 + g)))
                        else:
                            prs.append((p_tm(t, csl), Amat(g)))
                        if t > 0 and t - 1 >= tp0:
                            prs.append((p_tm(t - 1, csl), Amat(4 + g)))
                        elif t == 0 and pi > 0:
                            prs.append((p_prev(li, csl), Amat(4 + g)))
                        mm(ps(b, 128, off=i3 * 128), prs)
                    gsl = slice(grp[0] * 128, (grp[-1] + 1) * 128)
                    copy(dT(cc, gsl), ps(b, len(grp) * 128))
            copy(p_prev(li), p_tm(TP - 1), eng="pool")
            (wg4,) = fetch([(pgw[li].rearrange("g (k p) d -> p g k d", p=128), (4, 2, 256))])
            for oc in range(8):
                g, o2 = oc // 2, oc % 2
                for half, hs in enumerate(halves()):
                    b = bank()
                    mm(ps(b, hw()), [(wg4(g, k2, slice(o2 * 128, (o2 + 1) * 128)), dT(2 * g + k2, hs)) for k2 in range(2)])
                    act(ya(oc, hs), ps(b, hw()), AF.Copy, scale=pscale(slice(oc, oc + 1)))
            for q in range(4):
                c0 = 2 * PW + q * 256
                (w0,) = fetch([(wi[:, 0:8, c0:c0 + 256], (8, 256))])
                (w1,) = fetch([(wi[:, 8:16, c0:c0 + 256], (8, 256))])
                for t in range(t0, TP):
                    b = bank()
                    tsl = slice(t * 128, (t + 1) * 128)
                    mm(ps(b, 256), [(hT(k, tsl), (w0 if k < 8 else w1)(k % 8)) for k in range(KD)])
                    act(gv(t, slice(q * 256, (q + 1) * 256)), ps(b, 256), AF.Gelu_apprx_tanh)
            for t in range(t0, TP):
                ss = small(slice(t, t + 1))
                rs = small(slice(8 + t, 9 + t))
                act(jk_junk(), gv(t), AF.Square, accum=ss)
                act(rs, ss, AF.Sqrt, scale=1.0 / PW, bias=EPS)
                recip(rs, rs)
                stt(vn(t), gv(t), rs, vgain(), ALU.mult, ALU.mult)
            for h in range(8):
                c0 = PW + h * 128
                (wu_,) = fetch([(wi[:, :, c0:c0 + 128], (KD, 128))])
                hsl = slice(h * 128, (h + 1) * 128)
                for half, hs in enumerate(halves()):
                    bu = bank()
                    mm(ps(bu, hw()), [(wu_(k), hT(k, hs)) for k in range(KD)])
                    act(scr2(h % 2, hs), ps(bu, hw()), AF.Gelu_apprx_tanh)
                for grp in groups:
                    bs_ = bank()
                    for i3, t in enumerate(grp):
                        one_row = (ones_ap[0][0:1, :], ones_ap[1])
                        mm(ps(bs_, 128, off=i3 * 128),
                           [(one_row, b_hi(hsl)), (one_row, b_lo(hsl)), (vn(t, hsl), WmT(h))])
                    gsl = slice(grp[0] * 128, (grp[-1] + 1) * 128)
                    tt(gT(h, gsl), scr2(h % 2, gsl), ps(bs_, len(grp) * 128), ALU.mult)
            wpo = w_po[li].rearrange("(k p) n -> p k n", p=128)
            wso = w_so[li].rearrange("(k p) n -> p k n", p=128)
            for dc in range(KD):
                dsl = slice(dc * 128, (dc + 1) * 128)
                (wga,) = fetch([(wi[:, :, 3 * PW + dc * 128:3 * PW + (dc + 1) * 128], (KD, 128))])
                (wgb,) = fetch([(wi[:, :, 3 * PW + D + dc * 128:3 * PW + D + (dc + 1) * 128], (KD, 128))])
                wp_, ws_ = fetch([(wpo[:, :, dsl], (8, 128)), (wso[:, :, dsl], (8, 128))])
                for half, hs in enumerate(halves()):
                    b1, b2, b3, b4 = bank(), bank(), bank(), bank()
                    mm(ps(b1, hw()), [(wga(k), hT(k, hs)) for k in range(KD)])
                    mm(ps(b2, hw()), [(wp_(k), ya(k, hs)) for k in range(8)])
                    mm(ps(b3, hw()), [(wgb(k), hT(k, hs)) for k in range(KD)])
                    mm(ps(b4, hw()), [(ws_(k), gT(k, hs)) for k in range(8)])
                    s1, s2 = scr_next(), scr_next()
                    act(s1, ps(b1, hw()), AF.Sigmoid)
                    act(s2, ps(b3, hw()), AF.Sigmoid)
                    tt(s1, s1, ps(b2, hw()), ALU.mult)
                    tt(s2, s2, ps(b4, hw()), ALU.mult)
                    tt(mT(dc, hs), s1, s2, ALU.add)
            wo = w_o[li].rearrange("(k p) n -> p k n", p=128)
            def get_units(dc):
                (wo_,) = fetch([(wo[:, :, dc * 128:(dc + 1) * 128], (KD, 128))])
                return (wo_,)

            def mm_pairs(units, dc, hs):
                return [(units[0](k), mT(k, hs)) for k in range(KD)]

            def evac(dc, hs, pv):
                act(fT(dc, hs), pv, AF.Copy)

            tail(get_units, mm_pairs, evac, fin)

        def dump(name, view):
            if DEBUG.get("dump") != name:
                return
            ap, res = view
            n = int(np.prod(ap.shape[1:]))
            dst = dbg_d[0:ap.shape[0], 0:n]
            if len(ap.shape) == 3:
                dst = dst.rearrange("p (a b) -> p a b", a=ap.shape[1])
            S.dma("pool", [(dst, ap)], "DBG", reads=[res])
            S.wait_event("pool", ("DBG", S.cnt["DBG"]))

        xsrc = xT_d.rearrange("(k p) t -> p k t", p=128)
        odst = oT_d.rearrange("(k p) t -> p k t", p=128)
        gcols2 = [gcols, gcolsB]
        for pi in range(npass):
            tsl = slice(pi * G, (pi + 1) * G)
            S.dma("sp", [(xT()[0], xsrc[:, :, tsl])], "XLD", writes=[xT()[1]])
            subs = []
            for li in range(nl):
                sk = (0, 0, 0, 0)
                if pi == 0 and nl == DEPTH:
                    sk = HALO_SKIP[li]
                if pi == 0 and "skip" in DEBUG:
                    sk = DEBUG["skip"]
                subs += [("ffn", li, 0, sk[0], sk[0]), ("mix", li, 0, sk[1], sk[2]), ("ffn", li, 1, sk[3], sk[3])]

            def gview(li, gi):
                g = gcols2[li % 2]
                return lambda sl, g=g, gi=gi: g(gi, sl)

            def pre_of(sub):
                kind, li, which, pc0, bc0 = sub
                return ("ffn", gview(li, 0 if which == 0 else 4), pc0) if kind == "ffn" else ("mix", gview(li, 2), pc0)

            for si, sub in enumerate(subs):
                kind, li, which, pc0, bc0 = sub
                if si == 0:
                    g = gcols2[li % 2]
                    S.dma("sp", [(g()[0], gcols_d[li].rearrange("p (a b) -> p a b", a=6))], "PAR1_%d" % (li % 2), writes=[g()[1]])
                    st, gc, _ = pre_of(sub)
                    with at_c0(pc0):
                        for k in range(KD):
                            pre_chunk(k, st, gc)
                        pre_finish(st, gc)
                nxt = None
                if si + 1 < len(subs):
                    nsub = subs[si + 1]
                    if nsub[0] == "ffn" and nsub[2] == 0:
                        g = gcols2[nsub[1] % 2]
                        S.dma("sp", [(g()[0], gcols_d[nsub[1]].rearrange("p (a b) -> p a b", a=6))], "PAR1_%d" % (nsub[1] % 2), writes=[g()[1]])
                    nxt = pre_of(nsub)
                with at_c0(bc0):
                    if kind == "ffn":
                        ffn(li, which, (gview(li, 1 if which == 0 else 5), 0.5, nxt, bc0))
                    else:
                        mixer(li, pi, pc0 // 128, (gview(li, 3), 1.0, nxt, bc0))
            S.dma("sp", [(odst[:, :, tsl], xT()[0])], "XST", reads=[xT()[1]])
        S.wait_event("sp", ("XST", S.cnt["XST"]))
        block = es.enter_context(nc.Block())
        S.replay(block)
    return nc


POOL_WINDOWS = (2, 4, 8, 16)


def _const_tables(seq_start):
    import ml_dtypes
    A = np.zeros((16, 128, 128), np.float64)
    s = np.arange(128)[:, None]
    t = np.arange(128)[None, :]
    for g, w in enumerate(POOL_WINDOWS):
        A[g] = ((s <= t) & (s > t - w)) / w - (s == t)
        A[4 + g] = ((s - 128 > t - w)) / w
        cnt = np.minimum(t + 1, w)
        first = ((s <= t) & (s > t - w)) / (cnt if seq_start else w) - (s == t)
        hi = first.astype(np.float32).astype(ml_dtypes.bfloat16).astype(np.float64)
        A[8 + g] = hi
        A[12 + g] = first - hi
    Amat = np.ascontiguousarray(A.transpose(1, 0, 2).reshape(128, 16 * 128)).astype(np.float32)
    mask = np.tile((s <= t).astype(np.float32)[:, None, :], (1, 8, 1)).reshape(128, 1024)
    return Amat, np.ascontiguousarray(mask)


def _layer_params(inp, ls):
    def col(g):
        return np.ascontiguousarray(np.asarray(g)[ls].reshape(len(ls), KD, 128).transpose(0, 2, 1))

    gcols = np.concatenate([col(inp[k])[:, :, None, :] for k in
                            ("g_ffn1_pre", "g_ffn1_post", "g_mix_pre", "g_mix_post", "g_ffn2_pre", "g_ffn2_post")],
                           axis=2).reshape(len(ls), 128, 6 * KD)
    pscale = np.ascontiguousarray(np.asarray(inp["pool_scale"])[ls].reshape(len(ls), 8, 128).transpose(0, 2, 1))
    vgain = np.ascontiguousarray(np.broadcast_to(np.asarray(inp["sgu_v_gain"])[ls][:, None, :], (len(ls), 128, PW)))
    wsT = np.ascontiguousarray(np.asarray(inp["sgu_w_s"])[ls].transpose(0, 3, 1, 2).reshape(len(ls), 128, PW))
    bs = np.ascontiguousarray(np.asarray(inp["sgu_b_s"])[ls].reshape(len(ls), 1, PW))
    f = lambda k: np.ascontiguousarray(np.asarray(inp[k])[ls], dtype=np.float32)
    return {
        "w_up1": f("w_ffn1_up"), "w_dn1": f("w_ffn1_down"), "w_up2": f("w_ffn2_up"), "w_dn2": f("w_ffn2_down"),
        "w_in": f("w_in"), "pgw": f("pool_group_w"), "w_po": f("w_pool_out"), "w_so": f("w_sgu_out"),
        "w_o": f("w_out"), "gcols": np.ascontiguousarray(gcols, dtype=np.float32), "pscale": pscale.astype(np.float32),
        "vgain": vgain.astype(np.float32), "wsT": wsT.astype(np.float32), "bs": bs.astype(np.float32),
    }


def _shard_x(x):
    outs = []
    for c in range(N_CORES):
        b, half = c // 2, c % 2
        xs = np.zeros((NT * 128, D), np.float32)
        if half == 0:
            xs[256:] = x[b, 0:2048]
        else:
            xs[:] = x[b, 2048 - 256:4096]
        outs.append(np.ascontiguousarray(xs.T))
    return outs


_PROGS = {}


def _prog(nl):
    if nl not in _PROGS:
        _PROGS[nl] = build_program(nl)
    return _PROGS[nl]


def kernel(**inputs):
    x = np.asarray(inputs["x"], dtype=np.float32)
    tabs = [_const_tables(c % 2 == 0) for c in range(2)]
    mask = tabs[0][1]
    xs = _shard_x(x)
    groups = [list(range(DEPTH))] if FUSED else [[l] for l in range(DEPTH)]
    for ls in groups:
        par = _layer_params(inputs, ls)
        par["mask"] = mask
        nc = _prog(len(ls))
        in_maps = [dict(par, xT=xs[c], Amat=tabs[c % 2][0]) for c in range(N_CORES)]
        res = run_bass_kernel_spmd(nc, in_maps, core_ids=list(range(N_CORES)))
        xs = [np.asarray(r["oT"]) for r in res.results]
    out = np.zeros((4, 4096, D), np.float32)
    for c in range(N_CORES):
        b, half = c // 2, c % 2
        out[b, half * 2048:(half + 1) * 2048] = xs[c][:, 256:].T
    return out
```

```python
import numpy as np
from contextlib import ExitStack

import concourse.bass as bass
import concourse.mybir as mybir
from concourse.bass_utils import run_bass_kernel_spmd

F32 = mybir.dt.float32
BF16 = mybir.dt.bfloat16
AF = mybir.ActivationFunctionType
ALU = mybir.AluOpType

D = 2048
DFF = 5632
KD = D // 128
NJ = DFF // 128
NJH = NJ // 2
PW = 1024
DEPTH = 4
EPS = 1e-6
NT = 18
TP = 6
NPASS = NT // TP
G = TP * 128
H2 = G // 2
NSLOT = 6
SLOT_E = 2048
N_CORES = 8
FUSED = True
DEBUG = {}
HALO_SKIP = ((0, 0, 112, 112), (112, 0, 128, 128), (128, 128, 240, 240), (240, 128, 256, 256))


def _esz(dt):
    return 4 if dt is F32 else 2


class Region:
    def __init__(self, name, handle, nbytes):
        self.name, self.h, self.nbytes = name, handle, nbytes


class T:
    def __init__(self, region, dtype, off, dims, rows=(0, 128)):
        self.region, self.dtype, self.off, self.dims, self.rows = region, dtype, off, tuple(dims), rows
        es = _esz(dtype)
        n = int(np.prod(dims))
        assert off % 4 == 0 and off + n * es <= region.nbytes, (region.name, off, n * es, region.nbytes)
        base = region.h[rows[0]:rows[1], off // 2: off // 2 + n * es // 2]
        if dtype is F32:
            base = base.bitcast(F32)
        if len(dims) > 1:
            names = [f"a{i}" for i in range(len(dims))]
            pat = "p (" + " ".join(names) + ") -> p " + " ".join(names)
            base = base.rearrange(pat, **{nm: d for nm, d in zip(names[:-1], dims[:-1])})
        self.ap = base
        st = [1] * len(dims)
        for i in range(len(dims) - 2, -1, -1):
            st[i] = st[i + 1] * dims[i + 1]
        self.strides = st

    def __call__(self, *idx):
        idx = list(idx) + [slice(None)] * (len(self.dims) - len(idx))
        lo = hi = 0
        for i, d, s in zip(idx, self.dims, self.strides):
            if isinstance(i, int):
                a, b = i, i + 1
            else:
                a, b = i.indices(d)[:2]
            lo += a * s
            hi += (b - 1) * s
        es = _esz(self.dtype)
        res = (self.region.name, self.off + lo * es, self.off + (hi + 1) * es)
        return (self.ap[(slice(None),) + tuple(idx)], res)


class Sched:
    ENGS = ("pe", "act", "dve", "pool", "sp")

    def __init__(self, nc, es):
        self.nc, self.es = nc, es
        self.q = {e: [] for e in self.ENGS}
        self.sems = {}
        self.cnt = {}
        self.known = {e: {} for e in self.ENGS}
        self.segs = {}
        for e in self.ENGS:
            self.sem("E_" + e)

    def sem(self, name):
        if name not in self.sems:
            self.sems[name] = self.es.enter_context(self.nc.semaphore(name))
            self.cnt[name] = 0
        return self.sems[name]

    def add_region(self, name, nbytes):
        self.segs[name] = [[0, nbytes, None, {}]]

    def _range(self, res):
        name, s, e = res
        segs = self.segs[name]
        for x in (s, e):
            for i, sg in enumerate(segs):
                if sg[0] < x < sg[1]:
                    segs.insert(i + 1, [x, sg[1], sg[2], dict(sg[3])])
                    sg[1] = x
                    break
        return [sg for sg in segs if sg[0] >= s and sg[1] <= e]

    def _deps(self, eng, reads, writes):
        need = {}

        def add(ev):
            if ev is not None and need.get(ev[0], 0) < ev[1]:
                need[ev[0]] = ev[1]

        for r in reads:
            for sg in self._range(r):
                add(sg[2])
        for w in writes:
            for sg in self._range(w):
                add(sg[2])
                for s, v in sg[3].items():
                    add((s, v))
        waits = []
        for s, v in need.items():
            if eng == "pe" and s == "E_pe":
                continue
            if self.known[eng].get(s, 0) >= v:
                continue
            self.known[eng][s] = v
            waits.append((s, v))
        return waits

    def _commit(self, ev, reads, writes):
        for r in reads:
            for sg in self._range(r):
                if sg[3].get(ev[0], 0) < ev[1]:
                    sg[3][ev[0]] = ev[1]
        for w in writes:
            name, s, e = w
            self._range(w)
            segs = self.segs[name]
            keep = [sg for sg in segs if not (sg[0] >= s and sg[1] <= e)]
            keep.append([s, e, ev, {}])
            keep.sort(key=lambda sg: sg[0])
            self.segs[name] = keep

    def op(self, eng, fn, reads=(), writes=()):
        reads = [r for r in reads if r is not None]
        waits = self._deps(eng, reads, writes)
        s = "E_" + eng
        self.cnt[s] += 1
        ev = (s, self.cnt[s])
        sem = self.sems[s]
        self.q[eng].append((waits, lambda e, fn=fn, sem=sem: fn(e).then_inc(sem, 1)))
        self._commit(ev, reads, writes)

    def dma(self, eng, pairs, semname, reads=(), writes=()):
        sem = self.sem(semname)
        waits = self._deps(eng, reads, writes)
        self.cnt[semname] += 16 * len(pairs)
        ev = (semname, self.cnt[semname])

        def fn(e, pairs=pairs, sem=sem):
            for dst, src in pairs:
                e.dma_start(out=dst, in_=src).then_inc(sem, 16)

        self.q[eng].append((waits, fn))
        self._commit(ev, reads, writes)
        return ev

    def wait_event(self, eng, ev):
        self.q[eng].append(([ev], lambda e: None))

    def replay(self, block):
        sems = self.sems

        def run(e, items):
            for waits, fn in items:
                for s, v in waits:
                    e.wait_ge(sems[s], v)
                fn(e)

        @block.tensor
        def _(e):
            run(e, self.q["pe"])

        @block.scalar
        def _(e):
            run(e, self.q["act"])

        @block.vector
        def _(e):
            run(e, self.q["dve"])

        @block.gpsimd
        def _(e):
            run(e, self.q["pool"])

        @block.sync
        def _(e):
            run(e, self.q["sp"])


def build_program(nl, npass=NPASS, final_tiles=None):
    nt = npass * TP
    ntok = nt * 128
    nc = bass.Bass("TRN2", target_bir_lowering=False)

    def din(name, shape):
        return nc.dram_tensor(name, list(shape), F32, kind="ExternalInput").ap()

    xT_d = din("xT", (D, ntok))
    w_up = [din("w_up1", (nl, D, 2 * DFF)), din("w_up2", (nl, D, 2 * DFF))]
    w_dn = [din("w_dn1", (nl, DFF, D)), din("w_dn2", (nl, DFF, D))]
    w_in = din("w_in", (nl, D, 3 * PW + 2 * D))
    pgw = din("pgw", (nl, 4, 256, 256))
    w_po = din("w_po", (nl, PW, D))
    w_so = din("w_so", (nl, PW, D))
    w_o = din("w_o", (nl, D, D))
    gcols_d = din("gcols", (nl, 128, 6 * KD))
    pscale_d = din("pscale", (nl, 128, 8))
    vgain_d = din("vgain", (nl, 128, PW))
    wsT_d = din("wsT", (nl, 128, PW))
    bs_d = din("bs", (nl, 1, PW))
    A_d = din("Amat", (128, 16 * 128))
    mask_d = din("mask", (128, PW))
    oT_d = nc.dram_tensor("oT", [D, ntok], F32, kind="ExternalOutput").ap()
    dbg_d = nc.dram_tensor("dbg", [128, KD * G], F32, kind="ExternalOutput").ap() if DEBUG else None

    with ExitStack() as es:
        S = Sched(nc, es)

        def region(name, nbytes):
            h = es.enter_context(nc.sbuf_tensor(name, [128, nbytes // 2], BF16))
            S.add_region(name, nbytes)
            return Region(name, h, nbytes)

        R_x = region("xT_sb", KD * G * 4)
        R_h = region("hT_sb", KD * G * 2)
        R_f = region("fT_sb", KD * G * 4)
        R_big = region("big_sb", NJH * G * 2)
        R_w = region("wring", NSLOT * SLOT_E * 2)
        R_m = region("misc", 3072 + 3072 + 4 * 1536 + 3072 + 2048 + 2 * 384 + 32 + 256 + 64 + 2048 * nl + 224)
        R_b = region("brow", 2048)
        R_b2 = region("brow2", 2048)
        ps_h = es.enter_context(nc.psum_tensor("psum", [128, 4096], F32))
        S.add_region("psum", 16384)

        class PS:
            name, h, nbytes = "psum", ps_h, 16384

        def ps(bank, n, off=0):
            a, b = bank * 512 + off, bank * 512 + off + n
            return (ps_h[:, a:b], ("psum", a * 4, b * 4))

        xT = T(R_x, F32, 0, (KD, G))
        hT = T(R_h, BF16, 0, (KD, G))
        fT = T(R_f, F32, 0, (KD, G))

        vn = T(R_f, BF16, 0, (TP, PW))
        gT = T(R_f, BF16, 12288, (8, G))
        bhf_stage = T(R_f, F32, 12288, (PW,), rows=(0, 1))
        jk_junk = T(R_f, BF16, 24576, (PW,))
        vgain = T(R_f, F32, 26624, (PW,))
        Amat = T(R_f, BF16, 30720, (16, 128))
        ya = T(R_f, BF16, 36864, (8, G))
        wst_stage = T(R_f, F32, 36864, (PW,))
        mask_stage = T(R_f, F32, 40960, (PW,))
        b_stage = T(R_f, F32, 45056, (PW,), rows=(0, 1))
        actT = T(R_big, BF16, 0, (NJH, G))
        p_tm = T(R_big, BF16, 0, (TP, PW))
        dT = T(R_big, BF16, 12288, (8, G))
        gv = T(R_big, F32, 0, (TP, PW))
        mT = T(R_big, BF16, 0, (KD, G))
        mo = 0

        def misc(dtype, dims, rows=(0, 128)):
            nonlocal mo
            t = T(R_m, dtype, mo, dims, rows)
            mo += int(np.prod(dims)) * _esz(dtype)
            mo = (mo + 31) // 32 * 32
            return t

        rstd_bc = misc(F32, (G,))
        sq = misc(BF16, (2, G))
        scr = misc(F32, (4, H2))
        tpost = misc(F32, (G,))
        scr2 = T(R_m, F32, scr.off, (2, G))
        WmT = misc(BF16, (8, 128))
        gcols = misc(F32, (6, KD))
        gcolsB = misc(F32, (6, KD))
        pscale = misc(F32, (8,))
        ones = misc(BF16, (128,))
        small = misc(F32, (16,))
        p_prev = misc(BF16, (nl, PW))
        b_hi = T(R_b, BF16, 0, (PW,), rows=(0, 1))
        b_lo = T(R_b2, BF16, 0, (PW,), rows=(0, 1))
        wslot = T(R_w, BF16, 0, (NSLOT, SLOT_E))

        state = {"slot": 0, "bank": 0, "scr": 0, "sq": 0}

        def fetch(parts):
            slot = state["slot"] % NSLOT
            state["slot"] += 1
            off = 0
            views, pairs, writes = [], [], []
            for src, dims in parts:
                n = int(np.prod(dims))
                v = T(R_w, BF16, (slot * SLOT_E + off) * 2, dims)
                off += n
                assert off <= SLOT_E
                views.append(v)
                ap, res = v()
                pairs.append((ap, src))
                writes.append(res)
            S.dma("pool", pairs, f"W{slot}", writes=writes)
            return views

        def bank():
            b = state["bank"] % 6
            state["bank"] += 1
            return b

        def scr_next():
            i = state["scr"] % 4
            state["scr"] += 1
            return scr(i, slice(0, hw()))

        def mm(out, pairs, start=True, stop=True):
            reads = [x[1] for pr in pairs for x in pr]
            n = len(pairs)

            def fn(e, out=out, pairs=pairs, start=start, stop=stop):
                ins = None
                for i, (l, r) in enumerate(pairs):
                    ins = e.matmul(out[0], lhsT=l[0], rhs=r[0], start=(start and i == 0), stop=(stop and i == n - 1))
                return ins

            S.op("pe", fn, reads=reads, writes=[out[1]])

        def act(out, in_, func, scale=1.0, bias=0.0, accum=None, extra_reads=()):
            reads = [in_[1]] + [x[1] for x in extra_reads]
            writes = [out[1]] + ([accum[1]] if accum is not None else [])
            sc = scale[0] if isinstance(scale, tuple) else scale
            if isinstance(scale, tuple):
                reads.append(scale[1])
            kw = {}
            if accum is not None:
                kw["accum_out"] = accum[0]
            S.op("act", lambda e: e.activation(out=out[0], in_=in_[0], func=func, bias=bias, scale=sc, **kw), reads=reads, writes=writes)

        def tt(out, a, b, op, eng="dve"):
            S.op(eng, lambda e: e.tensor_tensor(out=out[0], in0=a[0], in1=b[0], op=op), reads=[a[1], b[1]], writes=[out[1]])

        def stt(out, in0, scalar, in1, op0, op1, eng="dve"):
            reads = [in0[1], in1[1]]
            sc = scalar
            if isinstance(scalar, tuple):
                reads.append(scalar[1])
                sc = scalar[0]
            S.op(eng, lambda e: e.scalar_tensor_tensor(out=out[0], in0=in0[0], scalar=sc, in1=in1[0], op0=op0, op1=op1), reads=reads, writes=[out[1]])

        def copy(out, in_, eng="dve"):
            S.op(eng, lambda e: e.tensor_copy(out=out[0], in_=in_[0]), reads=[in_[1]], writes=[out[1]])

        def recip(out, in_):
            S.op("dve", lambda e: e.reciprocal(out=out[0], in_=in_[0]), reads=[in_[1]], writes=[out[1]])

        ones_ap = ones()
        cur = {"c0": 0}

        def hw():
            return (G - cur["c0"]) // 2

        def halves():
            c0, h = cur["c0"], (G - cur["c0"]) // 2
            return [slice(c0, c0 + h), slice(c0 + h, G)]

        def FULL():
            return slice(cur["c0"], G)

        class at_c0:
            def __init__(self, c0):
                self.c0 = c0

            def __enter__(self):
                self.old = cur["c0"]
                cur["c0"] = self.c0

            def __exit__(self, *a):
                cur["c0"] = self.old

        S.op("dve", lambda e: e.memset(ones_ap[0], 1.0), writes=[ones_ap[1]])
        for li in range(nl):
            S.op("dve", lambda e, li=li: e.memset(p_prev(li)[0], 0.0), writes=[p_prev(li)[1]])

        def norm_stats_bank(half):
            return 6 + half

        def square_of(tv, idx):
            i = state["sq"] % 2
            state["sq"] += 1
            act(sq(i, FULL()), tv(idx, FULL()), AF.Square)
            return sq(i)

        def stats_mm(s, first, last):
            for half, hs in enumerate(halves()):
                mm(ps(6 + half, hw()), [(ones_ap, (s[0][:, hs], s[1]))], start=first, stop=last)

        def stats_of(tv, idx, first, last):
            stats_mm(square_of(tv, idx), first, last)

        def rstd_from_stats(weight=1.0):
            w2 = 1.0 / (weight * weight)
            for half, hs in enumerate(halves()):
                act(rstd_bc(hs), ps(6 + half, hw()), AF.Sqrt, scale=w2 / D, bias=w2 * EPS)
            recip(rstd_bc(FULL()), rstd_bc(FULL()))

        def pre_chunk(k, style, gc):
            if style == "ffn":
                act(hT(k, FULL()), xT(k, FULL()), AF.Copy, scale=gc(slice(k, k + 1)))
            stats_of(xT, k, k == 0, k == KD - 1)

        def pre_finish(style, gc):
            rstd_from_stats(1.0)
            if style == "mix":
                for k in range(KD):
                    stt(hT(k, FULL()), xT(k, FULL()), gc(slice(k, k + 1)), rstd_bc(FULL()), ALU.mult, ALU.mult)

        def boundary(gpost, weight, nxt, c0):
            with at_c0(c0):
                rstd_from_stats(weight)
            tbs = [lambda sl: tpost(sl), lambda sl: scr2(0, sl), lambda sl: scr2(1, sl)]
            for k in range(KD + 1):
                with at_c0(c0):
                    if k < KD:
                        stt(tbs[k % 3](FULL()), fT(k, FULL()), gpost(slice(k, k + 1)), rstd_bc(FULL()), ALU.mult, ALU.mult)
                    if k > 0:
                        tt(xT(k - 1, FULL()), xT(k - 1, FULL()), tbs[(k - 1) % 3](FULL()), ALU.add)
                if k > 0 and nxt is not None:
                    with at_c0(nxt[2]):
                        pre_chunk(k - 1, nxt[0], nxt[1])
            if nxt is not None:
                with at_c0(nxt[2]):
                    pre_finish(nxt[0], nxt[1])

        PRO_N = 3

        class FfnPrologue:
            def __init__(self, li, which):
                self.wu = w_up[which][li].rearrange("(k p) n -> p k n", p=128)
                self.units = {}

            def fetch_j(self, jj):
                (wg,) = fetch([(self.wu[:, :, jj * 128:(jj + 1) * 128], (KD, 128))])
                (wv,) = fetch([(self.wu[:, :, DFF + jj * 128:DFF + (jj + 1) * 128], (KD, 128))])
                self.units[jj] = (wg, wv)

            def groups(self, jj, half):
                wg, wv = self.units[jj]
                hs = halves()[half]
                bg, bu = bank(), bank()
                mm(ps(bg, hw()), [(wg(k), hT(k, hs)) for k in range(KD)])
                mm(ps(bu, hw()), [(wv(k), hT(k, hs)) for k in range(KD)])
                return bg, bu

            def evac(self, jj, half, banks):
                bg, bu = banks
                hs = halves()[half]
                gs, us = scr(2, slice(0, hw())), scr(3, slice(0, hw()))
                tt(gs, ps(bg, hw()), rstd_bc(hs), ALU.mult)
                act(gs, gs, AF.Silu)
                tt(us, ps(bu, hw()), rstd_bc(hs), ALU.mult)
                tt(actT(jj, hs), gs, us, ALU.mult)

        def tail(get_units, mm_pairs, evac, fin):
            gpost, weight, nxt, c0 = fin
            split = nxt is not None and nxt[2] == c0
            hv = halves()
            h_ = hw()
            w2 = 1.0 / (weight * weight)
            tbs = [lambda sl: tpost(sl), lambda sl: scr2(0, sl), lambda sl: scr2(1, sl)]
            st = {"pend": None, "n": 0}

            def sq_half(tv, idx, half):
                i = state["sq"] % 2
                state["sq"] += 1
                act(sq(i, hv[half]), tv(idx, hv[half]), AF.Square)
                return sq(i)

            def smm_half(sv, half, first, last):
                mm(ps(6 + half, h_), [(ones_ap, (sv[0][:, hv[half]], sv[1]))], start=first, stop=last)

            def flush():
                if st["pend"] is not None:
                    sv, half, k = st["pend"]
                    smm_half(sv, half, k == 0, k == KD - 1)
                    st["pend"] = None

            def chunk_update(k, half):
                hs = hv[half]
                tb = tbs[st["n"] % len(tbs)](hs)
                st["n"] += 1
                stt(tb, fT(k, hs), gpost(slice(k, k + 1)), rstd_bc(hs), ALU.mult, ALU.mult)
                tt(xT(k, hs), xT(k, hs), tb, ALU.add)
                flush()
                if nxt[0] == "ffn":
                    act(hT(k, hs), xT(k, hs), AF.Copy, scale=nxt[1](slice(k, k + 1)))
                st["pend"] = (sq_half(xT, k, half), half, k)

            def rstd_half(half, sc, bi):
                act(rstd_bc(hv[half]), ps(6 + half, h_), AF.Sqrt, scale=sc, bias=bi)
                recip(rstd_bc(hv[half]), rstd_bc(hv[half]))

            HM = 3 if split else 0
            for dc in range(KD - HM):
                units = get_units(dc)
                svs = [sq_half(fT, dc - 1, half) for half in range(2)] if dc > 0 else None
                for half in range(2):
                    b = bank()
                    mm(ps(b, h_), mm_pairs(units, dc, hv[half]))
                    evac(dc, hv[half], ps(b, h_))
                if svs is not None:
                    for half in range(2):
                        smm_half(svs[half], half, dc == 1, False)
            if HM == 0:
                for half in range(2):
                    smm_half(sq_half(fT, KD - 1, half), half, False, True)
                    rstd_half(half, w2 / D, w2 * EPS)
            else:
                tail_units = [get_units(dc) for dc in range(KD - HM, KD)]
                nupd = 0
                for half in range(2):
                    for i, dc in enumerate(range(KD - HM, KD)):
                        sv = sq_half(fT, dc - 1, half)
                        b = bank()
                        mm(ps(b, h_), mm_pairs(tail_units[i], dc, hv[half]))
                        evac(dc, hv[half], ps(b, h_))
                        smm_half(sv, half, dc == 1, False)
                        if half == 1 and i >= 1:
                            for _ in range(2 if i == 1 else 3):
                                chunk_update(nupd, 0)
                                nupd += 1
                    smm_half(sq_half(fT, KD - 1, half), half, False, True)
                    rstd_half(half, w2 / D, w2 * EPS)
            LAG = KD - (nupd if HM else 0)
            if split:
                for k in range(KD - LAG, KD):
                    chunk_update(k, 0)
                flush()
                rstd_half(0, 1.0 / D, EPS)
                pro = nxt[3] if len(nxt) > 3 else None
                if pro is None:
                    for k in range(KD):
                        chunk_update(k, 1)
                    flush()
                    rstd_half(1, 1.0 / D, EPS)
                else:
                    tbs[:] = [lambda sl: tpost(sl), lambda sl: scr2(0, sl)]
                    kb = 0
                    for jj, nb in zip(range(PRO_N), (5, 5, 6)):
                        pro.fetch_j(jj)
                        banks = pro.groups(jj, 0)
                        for _ in range(nb):
                            chunk_update(kb, 1)
                            kb += 1
                        pro.evac(jj, 0, banks)
                    flush()
                    rstd_half(1, 1.0 / D, EPS)
                    for jj in range(PRO_N):
                        banks = pro.groups(jj, 1)
                        pro.evac(jj, 1, banks)
                if nxt[0] == "mix":
                    for k in range(KD):
                        stt(hT(k, FULL()), xT(k, FULL()), nxt[1](slice(k, k + 1)), rstd_bc(FULL()), ALU.mult, ALU.mult)
            else:
                for k in range(KD + 1):
                    if k < KD:
                        stt(tbs[k % 3](FULL()), fT(k, FULL()), gpost(slice(k, k + 1)), rstd_bc(FULL()), ALU.mult, ALU.mult)
                    if k > 0:
                        tt(xT(k - 1, FULL()), xT(k - 1, FULL()), tbs[(k - 1) % 3](FULL()), ALU.add)
                        if nxt is not None:
                            with at_c0(nxt[2]):
                                pre_chunk(k - 1, nxt[0], nxt[1])
                if nxt is not None:
                    with at_c0(nxt[2]):
                        pre_finish(nxt[0], nxt[1])

        def ffn(li, which, fin, pro=None):
            wu = w_up[which][li].rearrange("(k p) n -> p k n", p=128)
            wd = w_dn[which][li].rearrange("(j p) n -> p j n", p=128)
            for fh in range(2):
                for jj in range(NJH):
                    if pro is not None and fh == 0 and jj < PRO_N:
                        continue
                    j = fh * NJH + jj
                    (wg,) = fetch([(wu[:, :, j * 128:(j + 1) * 128], (KD, 128))])
                    (wv,) = fetch([(wu[:, :, DFF + j * 128:DFF + (j + 1) * 128], (KD, 128))])
                    for half, hs in enumerate(halves()):
                        bg, bu = bank(), bank()
                        mm(ps(bg, hw()), [(wg(k), hT(k, hs)) for k in range(KD)])
                        mm(ps(bu, hw()), [(wv(k), hT(k, hs)) for k in range(KD)])
                        gs, us = scr_next(), scr_next()
                        tt(gs, ps(bg, hw()), rstd_bc(hs), ALU.mult)
                        act(gs, gs, AF.Silu)
                        tt(us, ps(bu, hw()), rstd_bc(hs), ALU.mult)
                        tt(actT(jj, hs), gs, us, ALU.mult)
                if fh == 0:
                    for dc in range(KD):
                        (wa,) = fetch([(wd[:, 0:11, dc * 128:(dc + 1) * 128], (11, 128))])
                        (wb,) = fetch([(wd[:, 11:22, dc * 128:(dc + 1) * 128], (11, 128))])
                        for half, hs in enumerate(halves()):
                            b = bank()
                            mm(ps(b, hw()), [((wa if jj < 11 else wb)(jj % 11), actT(jj, hs)) for jj in range(NJH)])
                            act(fT(dc, hs), ps(b, hw()), AF.Copy)
                else:
                    def get_units(dc):
                        (wa,) = fetch([(wd[:, NJH:NJH + 11, dc * 128:(dc + 1) * 128], (11, 128))])
                        (wb,) = fetch([(wd[:, NJH + 11:NJH + 22, dc * 128:(dc + 1) * 128], (11, 128))])
                        return wa, wb

                    def mm_pairs(units, dc, hs):
                        return [((units[0] if jj < 11 else units[1])(jj % 11), actT(jj, hs)) for jj in range(NJH)]

                    def evac(dc, hs, pv):
                        tt(fT(dc, hs), fT(dc, hs), pv, ALU.add)

                    tail(get_units, mm_pairs, evac, fin)

        def mixer(li, pi, tp0, fin):
            t0 = cur["c0"] // 128
            groups = [g_ for g_ in (list(range(t0, 3)), list(range(max(t0, 3), TP))) if g_]
            wi = w_in[li].rearrange("(k p) n -> p k n", p=128)
            S.dma("sp", [(pscale()[0], pscale_d[li]), (vgain()[0], vgain_d[li]), (wst_stage()[0], wsT_d[li]),
                         (mask_stage()[0], mask_d), (b_stage()[0], bs_d[li])], "PAR2",
                  writes=[pscale()[1], vgain()[1], wst_stage()[1], mask_stage()[1], b_stage()[1]])
            S.dma("pool", [(Amat()[0], A_d.rearrange("p (a b) -> p a b", a=16))], "PAR0", writes=[Amat()[1]])
            wm_flat = T(R_m, BF16, WmT.off, (PW,))
            tt(wm_flat(), wst_stage(), mask_stage(), ALU.mult)
            copy(b_hi(), b_stage())
            copy(bhf_stage(), b_hi())
            tt(b_lo(), b_stage(), bhf_stage(), ALU.subtract)
            for q in range(4):
                (w0,) = fetch([(wi[:, 0:8, q * 256:(q + 1) * 256], (8, 256))])
                (w1,) = fetch([(wi[:, 8:16, q * 256:(q + 1) * 256], (8, 256))])
                for t in range(tp0, TP):
                    b = bank()
                    tsl = slice(t * 128, (t + 1) * 128)
                    mm(ps(b, 256), [(hT(k, tsl), (w0 if k < 8 else w1)(k % 8)) for k in range(KD)])
                    act(p_tm(t, slice(q * 256, (q + 1) * 256)), ps(b, 256), AF.Copy)
            for cc in range(8):
                g = cc // 2
                csl = slice(cc * 128, (cc + 1) * 128)
                for grp in groups:
                    b = bank()
                    for i3, t in enumerate(grp):
                        gt = pi * TP + t
                        prs = []
                        if gt == 2:
                            prs.append((p_tm(t, csl), Amat(8 + g)))
                            prs.append((p_tm(t, csl), Amat(12 + g)))
                        else:
                            prs.append((p_tm(t, csl), Amat(g)))
                        if t > 0 and t - 1 >= tp0:
                            prs.append((p_tm(t - 1, csl), Amat(4 + g)))
                        elif t == 0 and pi > 0:
                            prs.append((p_prev(li, csl), Amat(4 + g)))
                        mm(ps(b, 128, off=i3 * 128), prs)
                    gsl = slice(grp[0] * 128, (grp[-1] + 1) * 128)
                    copy(dT(cc, gsl), ps(b, len(grp) * 128))
            copy(p_prev(li), p_tm(TP - 1), eng="pool")
            (wg4,) = fetch([(pgw[li].rearrange("g (k p) d -> p g k d", p=128), (4, 2, 256))])
            for oc in range(8):
                g, o2 = oc // 2, oc % 2
                for half, hs in enumerate(halves()):
                    b = bank()
                    mm(ps(b, hw()), [(wg4(g, k2, slice(o2 * 128, (o2 + 1) * 128)), dT(2 * g + k2, hs)) for k2 in range(2)])
                    act(ya(oc, hs), ps(b, hw()), AF.Copy, scale=pscale(slice(oc, oc + 1)))
            for q in range(4):
                c0 = 2 * PW + q * 256
                (w0,) = fetch([(wi[:, 0:8, c0:c0 + 256], (8, 256))])
                (w1,) = fetch([(wi[:, 8:16, c0:c0 + 256], (8, 256))])
                for t in range(t0, TP):
                    b = bank()
                    tsl = slice(t * 128, (t + 1) * 128)
                    mm(ps(b, 256), [(hT(k, tsl), (w0 if k < 8 else w1)(k % 8)) for k in range(KD)])
                    act(gv(t, slice(q * 256, (q + 1) * 256)), ps(b, 256), AF.Gelu_apprx_tanh)
            for t in range(t0, TP):
                ss = small(slice(t, t + 1))
                rs = small(slice(8 + t, 9 + t))
                act(jk_junk(), gv(t), AF.Square, accum=ss)
                act(rs, ss, AF.Sqrt, scale=1.0 / PW, bias=EPS)
                recip(rs, rs)
                stt(vn(t), gv(t), rs, vgain(), ALU.mult, ALU.mult)
            for h in range(8):
                c0 = PW + h * 128
                (wu_,) = fetch([(wi[:, :, c0:c0 + 128], (KD, 128))])
                hsl = slice(h * 128, (h + 1) * 128)
                for half, hs in enumerate(halves()):
                    bu = bank()
                    mm(ps(bu, hw()), [(wu_(k), hT(k, hs)) for k in range(KD)])
                    act(scr2(h % 2, hs), ps(bu, hw()), AF.Gelu_apprx_tanh)
                for grp in groups:
                    bs_ = bank()
                    for i3, t in enumerate(grp):
                        one_row = (ones_ap[0][0:1, :], ones_ap[1])
                        mm(ps(bs_, 128, off=i3 * 128),
                           [(one_row, b_hi(hsl)), (one_row, b_lo(hsl)), (vn(t, hsl), WmT(h))])
                    gsl = slice(grp[0] * 128, (grp[-1] + 1) * 128)
                    tt(gT(h, gsl), scr2(h % 2, gsl), ps(bs_, len(grp) * 128), ALU.mult)
            wpo = w_po[li].rearrange("(k p) n -> p k n", p=128)
            wso = w_so[li].rearrange("(k p) n -> p k n", p=128)
            for dc in range(KD):
                dsl = slice(dc * 128, (dc + 1) * 128)
                (wga,) = fetch([(wi[:, :, 3 * PW + dc * 128:3 * PW + (dc + 1) * 128], (KD, 128))])
                (wgb,) = fetch([(wi[:, :, 3 * PW + D + dc * 128:3 * PW + D + (dc + 1) * 128], (KD, 128))])
                wp_, ws_ = fetch([(wpo[:, :, dsl], (8, 128)), (wso[:, :, dsl], (8, 128))])
                for half, hs in enumerate(halves()):
                    b1, b2, b3, b4 = bank(), bank(), bank(), bank()
                    mm(ps(b1, hw()), [(wga(k), hT(k, hs)) for k in range(KD)])
                    mm(ps(b2, hw()), [(wp_(k), ya(k, hs)) for k in range(8)])
                    mm(ps(b3, hw()), [(wgb(k), hT(k, hs)) for k in range(KD)])
                    mm(ps(b4, hw()), [(ws_(k), gT(k, hs)) for k in range(8)])
                    s1, s2 = scr_next(), scr_next()
                    act(s1, ps(b1, hw()), AF.Sigmoid)
                    act(s2, ps(b3, hw()), AF.Sigmoid)
                    tt(s1, s1, ps(b2, hw()), ALU.mult)
                    tt(s2, s2, ps(b4, hw()), ALU.mult)
                    tt(mT(dc, hs), s1, s2, ALU.add)
            wo = w_o[li].rearrange("(k p) n -> p k n", p=128)
            def get_units(dc):
                (wo_,) = fetch([(wo[:, :, dc * 128:(dc + 1) * 128], (KD, 128))])
                return (wo_,)

            def mm_pairs(units, dc, hs):
                return [(units[0](k), mT(k, hs)) for k in range(KD)]

            def evac(dc, hs, pv):
                act(fT(dc, hs), pv, AF.Copy)

            tail(get_units, mm_pairs, evac, fin)

        def dump(name, view):
            if DEBUG.get("dump") != name:
                return
            ap, res = view
            n = int(np.prod(ap.shape[1:]))
            dst = dbg_d[0:ap.shape[0], 0:n]
            if len(ap.shape) == 3:
                dst = dst.rearrange("p (a b) -> p a b", a=ap.shape[1])
            S.dma("pool", [(dst, ap)], "DBG", reads=[res])
            S.wait_event("pool", ("DBG", S.cnt["DBG"]))

        xsrc = xT_d.rearrange("(k p) t -> p k t", p=128)
        odst = oT_d.rearrange("(k p) t -> p k t", p=128)
        gcols2 = [gcols, gcolsB]
        for pi in range(npass):
            tsl = slice(pi * G, (pi + 1) * G)
            S.dma("sp", [(xT()[0], xsrc[:, :, tsl])], "XLD", writes=[xT()[1]])
            subs = []
            for li in range(nl):
                sk = (0, 0, 0, 0)
                if pi == 0 and nl == DEPTH:
                    sk = HALO_SKIP[li]
                if pi == 0 and "skip" in DEBUG:
                    sk = DEBUG["skip"]
                subs += [("ffn", li, 0, sk[0], sk[0]), ("mix", li, 0, sk[1], sk[2]), ("ffn", li, 1, sk[3], sk[3])]

            def gview(li, gi):
                g = gcols2[li % 2]
                return lambda sl, g=g, gi=gi: g(gi, sl)

            def pre_of(sub):
                kind, li, which, pc0, bc0 = sub
                return ("ffn", gview(li, 0 if which == 0 else 4), pc0) if kind == "ffn" else ("mix", gview(li, 2), pc0)

            next_pro = None
            for si, sub in enumerate(subs):
                kind, li, which, pc0, bc0 = sub
                if si == 0:
                    g = gcols2[li % 2]
                    S.dma("sp", [(g()[0], gcols_d[li].rearrange("p (a b) -> p a b", a=6))], "PAR1_%d" % (li % 2), writes=[g()[1]])
                    st, gc, _ = pre_of(sub)
                    with at_c0(pc0):
                        for k in range(KD):
                            pre_chunk(k, st, gc)
                        pre_finish(st, gc)
                nxt = None
                if si + 1 < len(subs):
                    nsub = subs[si + 1]
                    if nsub[0] == "ffn" and nsub[2] == 0:
                        g = gcols2[nsub[1] % 2]
                        S.dma("sp", [(g()[0], gcols_d[nsub[1]].rearrange("p (a b) -> p a b", a=6))], "PAR1_%d" % (nsub[1] % 2), writes=[g()[1]])
                    nxt = pre_of(nsub)
                    if nsub[0] == "ffn" and nsub[3] == bc0 and nsub[4] == bc0:
                        nxt = nxt + (FfnPrologue(nsub[1], nsub[2]),)
                my_pro, next_pro = next_pro, (nxt[3] if nxt is not None and len(nxt) > 3 else None)
                with at_c0(bc0):
                    if kind == "ffn":
                        ffn(li, which, (gview(li, 1 if which == 0 else 5), 0.5, nxt, bc0), my_pro)
                    else:
                        mixer(li, pi, pc0 // 128, (gview(li, 3), 1.0, nxt, bc0))
            S.dma("sp", [(odst[:, :, tsl], xT()[0])], "XST", reads=[xT()[1]])
        S.wait_event("sp", ("XST", S.cnt["XST"]))
        block = es.enter_context(nc.Block())
        S.replay(block)
    return nc


POOL_WINDOWS = (2, 4, 8, 16)


def _const_tables(seq_start):
    import ml_dtypes
    A = np.zeros((16, 128, 128), np.float64)
    s = np.arange(128)[:, None]
    t = np.arange(128)[None, :]
    for g, w in enumerate(POOL_WINDOWS):
        A[g] = ((s <= t) & (s > t - w)) / w - (s == t)
        A[4 + g] = ((s - 128 > t - w)) / w
        cnt = np.minimum(t + 1, w)
        first = ((s <= t) & (s > t - w)) / (cnt if seq_start else w) - (s == t)
        hi = first.astype(np.float32).astype(ml_dtypes.bfloat16).astype(np.float64)
        A[8 + g] = hi
        A[12 + g] = first - hi
    Amat = np.ascontiguousarray(A.transpose(1, 0, 2).reshape(128, 16 * 128)).astype(np.float32)
    mask = np.tile((s <= t).astype(np.float32)[:, None, :], (1, 8, 1)).reshape(128, 1024)
    return Amat, np.ascontiguousarray(mask)


def _layer_params(inp, ls):
    def col(g):
        return np.ascontiguousarray(np.asarray(g)[ls].reshape(len(ls), KD, 128).transpose(0, 2, 1))

    gcols = np.concatenate([col(inp[k])[:, :, None, :] for k in
                            ("g_ffn1_pre", "g_ffn1_post", "g_mix_pre", "g_mix_post", "g_ffn2_pre", "g_ffn2_post")],
                           axis=2).reshape(len(ls), 128, 6 * KD)
    pscale = np.ascontiguousarray(np.asarray(inp["pool_scale"])[ls].reshape(len(ls), 8, 128).transpose(0, 2, 1))
    vgain = np.ascontiguousarray(np.broadcast_to(np.asarray(inp["sgu_v_gain"])[ls][:, None, :], (len(ls), 128, PW)))
    wsT = np.ascontiguousarray(np.asarray(inp["sgu_w_s"])[ls].transpose(0, 3, 1, 2).reshape(len(ls), 128, PW))
    bs = np.ascontiguousarray(np.asarray(inp["sgu_b_s"])[ls].reshape(len(ls), 1, PW))
    f = lambda k: np.ascontiguousarray(np.asarray(inp[k])[ls], dtype=np.float32)
    return {
        "w_up1": f("w_ffn1_up"), "w_dn1": f("w_ffn1_down"), "w_up2": f("w_ffn2_up"), "w_dn2": f("w_ffn2_down"),
        "w_in": f("w_in"), "pgw": f("pool_group_w"), "w_po": f("w_pool_out"), "w_so": f("w_sgu_out"),
        "w_o": f("w_out"), "gcols": np.ascontiguousarray(gcols, dtype=np.float32), "pscale": pscale.astype(np.float32),
        "vgain": vgain.astype(np.float32), "wsT": wsT.astype(np.float32), "bs": bs.astype(np.float32),
    }


def _shard_x(x):
    outs = []
    for c in range(N_CORES):
        b, half = c // 2, c % 2
        xs = np.zeros((NT * 128, D), np.float32)
        if half == 0:
            xs[256:] = x[b, 0:2048]
        else:
            xs[:] = x[b, 2048 - 256:4096]
        outs.append(np.ascontiguousarray(xs.T))
    return outs


_PROGS = {}


def _prog(nl):
    if nl not in _PROGS:
        _PROGS[nl] = build_program(nl)
    return _PROGS[nl]


def kernel(**inputs):
    x = np.asarray(inputs["x"], dtype=np.float32)
    tabs = [_const_tables(c % 2 == 0) for c in range(2)]
    mask = tabs[0][1]
    xs = _shard_x(x)
    groups = [list(range(DEPTH))] if FUSED else [[l] for l in range(DEPTH)]
    for ls in groups:
        par = _layer_params(inputs, ls)
        par["mask"] = mask
        nc = _prog(len(ls))
        in_maps = [dict(par, xT=xs[c], Amat=tabs[c % 2][0]) for c in range(N_CORES)]
        res = run_bass_kernel_spmd(nc, in_maps, core_ids=list(range(N_CORES)))
        xs = [np.asarray(r["oT"]) for r in res.results]
    out = np.zeros((4, 4096, D), np.float32)
    for c in range(N_CORES):
        b, half = c // 2, c % 2
        out[b, half * 2048:(half + 1) * 2048] = xs[c][:, 256:].T
    return out
```
